# Optimizing a Trainium2 kernel written in Bass

```python
import jax, jax.numpy as jnp
from jax import lax
import numpy as np

D_MODEL = 1024
BATCH = 2
SEQ = 8192
DEPTH = 1

N_MEM = 256
XATTN_HEADS = 4
XATTN_HEAD_DIM = D_MODEL // XATTN_HEADS
D_FF = 2816
GLA_HEADS = 4
GLA_DK = 64
GLA_DV = 128
GLA_KEY_W = GLA_HEADS * GLA_DK
GLA_VAL_W = GLA_HEADS * GLA_DV
GATE_RANK = 16
GATE_TAU = 16.0
CHUNK = 64
CONV_W = 512
CONV_K = 3
D_MIX = GLA_VAL_W + CONV_W
EPS = 1e-6
IN_SPLITS = (GLA_KEY_W, GLA_KEY_W, GLA_VAL_W, GLA_VAL_W, GATE_RANK, GATE_RANK,
             CONV_W, CONV_W, CONV_W)
D_IN = sum(IN_SPLITS)

kernel_name = "hybrid_gla_shortconv_macaron_memxattn"


def rms_norm(x, g):
    xf = x.astype(jnp.float32)
    y = xf * lax.rsqrt(jnp.mean(xf * xf, axis=-1, keepdims=True) + EPS)
    return (y * g.astype(jnp.float32)).astype(x.dtype)


def swiglu(h, w_gate, w_up, w_down):
    return (jax.nn.silu(h @ w_gate) * (h @ w_up)) @ w_down


def gla_chunked(q, k, v, g):
    b_, h_, s_, dk = q.shape
    dv = v.shape[-1]
    n = s_ // CHUNK
    q = q.reshape(b_, h_, n, CHUNK, dk)
    k = k.reshape(b_, h_, n, CHUNK, dk)
    v = v.reshape(b_, h_, n, CHUNK, dv)
    bcum = jnp.cumsum(g.reshape(b_, h_, n, CHUNK, dk), axis=3)
    b_last = bcum[:, :, :, -1:, :]
    q_in = q * jnp.exp(bcum)
    k_in = k * jnp.exp(-bcum)
    mask = jnp.tril(jnp.ones((CHUNK, CHUNK), dtype=bool))
    att = jnp.einsum('bhncd,bhnmd->bhncm', q_in, k_in)
    att = jnp.where(mask, att, 0.0)
    o_intra = jnp.einsum('bhncm,bhnmv->bhncv', att, v)
    k_st = k * jnp.exp(b_last - bcum)
    d_state = jnp.einsum('bhncd,bhncv->bhndv', k_st, v)
    decay = jnp.exp(b_last[:, :, :, 0, :])

    def step(state, inp):
        dec, ds = inp
        return dec[..., None] * state + ds, state

    init = jnp.zeros((b_, h_, dk, dv), jnp.float32)
    _, states = lax.scan(step, init, (jnp.moveaxis(decay, 2, 0), jnp.moveaxis(d_state, 2, 0)))
    states = jnp.moveaxis(states, 0, 2)
    o_inter = jnp.einsum('bhncd,bhndv->bhncv', q_in, states)
    return (o_intra + o_inter).reshape(b_, h_, s_, dv)


def low_rank_log_gate(z, w2, b2):
    zf = z.astype(jnp.float32) @ w2.astype(jnp.float32) + b2.astype(jnp.float32)
    lg = jax.nn.log_sigmoid(zf) / GATE_TAU
    bsz, s_ = z.shape[0], z.shape[1]
    return lg.reshape(bsz, s_, GLA_HEADS, GLA_DK).transpose(0, 2, 1, 3)


def to_heads(t, nh):
    bsz, s_, w = t.shape
    return t.reshape(bsz, s_, nh, w // nh).transpose(0, 2, 1, 3)


def token_mixing(h, w_in, gf_w, gf_b, gb_w, gb_b, gla_norm, conv_w, conv_b, w_out):
    bsz, s_, _ = h.shape
    proj = h @ w_in
    idx = np.cumsum(IN_SPLITS)[:-1].tolist()
    q, k, v, r, zf, zb, bg, cg, xv = jnp.split(proj, idx, axis=-1)
    qh = to_heads(q, GLA_HEADS).astype(jnp.float32) * (GLA_DK ** -0.5)
    kh = to_heads(k, GLA_HEADS).astype(jnp.float32)
    vh = to_heads(v, GLA_HEADS).astype(jnp.float32)
    g_fwd = low_rank_log_gate(zf, gf_w, gf_b)
    g_bwd = low_rank_log_gate(zb, gb_w, gb_b)
    o_fwd = gla_chunked(qh, kh, vh, g_fwd)
    flip = lambda t: jnp.flip(t, axis=2)
    o_bwd = flip(gla_chunked(flip(qh), flip(kh), flip(vh), flip(g_bwd)))
    o = o_fwd + o_bwd
    o = o * lax.rsqrt(jnp.mean(o * o, axis=-1, keepdims=True) + EPS)
    o = o * gla_norm.astype(jnp.float32).reshape(GLA_HEADS, 1, GLA_DV)
    o = o.transpose(0, 2, 1, 3).reshape(bsz, s_, GLA_VAL_W).astype(h.dtype)
    a_out = o * jax.nn.silu(r)
    u = cg * xv
    u_prev = jnp.pad(u[:, :-1], ((0, 0), (1, 0), (0, 0)))
    u_next = jnp.pad(u[:, 1:], ((0, 0), (0, 1), (0, 0)))
    conv = conv_w[0] * u_prev + conv_w[1] * u + conv_w[2] * u_next + conv_b
    c_out = bg * conv
    return jnp.concatenate([a_out, c_out], axis=-1) @ w_out


def memory_cross_attention(h, m, w_q, w_kv, w_o):
    q = to_heads(h @ w_q, XATTN_HEADS)
    kv = m @ w_kv
    k, v = jnp.split(kv, 2, axis=-1)
    k = to_heads(k, XATTN_HEADS)
    v = to_heads(v, XATTN_HEADS)
    s = jnp.einsum('bhsd,bhmd->bhsm', q.astype(jnp.float32), k.astype(jnp.float32))
    p = jax.nn.softmax(s * (XATTN_HEAD_DIM ** -0.5), axis=-1)
    o = jnp.einsum('bhsm,bhmd->bhsd', p, v.astype(jnp.float32)).astype(h.dtype)
    bsz, _, s_, _ = o.shape
    o = o.transpose(0, 2, 1, 3).reshape(bsz, s_, D_MODEL)
    return o @ w_o


def setup_inputs(seed: int = 0) -> dict:
    key = jax.random.key(seed)
    ks = iter(jax.random.split(key, 40))

    def dense(shape, fan_in):
        return jax.random.normal(next(ks), shape, jnp.float32) * (fan_in ** -0.5)

    def gain(shape):
        return 1.0 + 0.02 * jax.random.normal(next(ks), shape, jnp.float32)

    def small(shape, scale=0.02, offset=0.0):
        return offset + scale * jax.random.normal(next(ks), shape, jnp.float32)

    L = DEPTH
    return {
        "x": jax.random.normal(next(ks), (BATCH, SEQ, D_MODEL), jnp.float32),
        "mem": jax.random.normal(next(ks), (BATCH, N_MEM, D_MODEL), jnp.float32),
        "ffn1_norm": gain((L, D_MODEL)),
        "ffn1_w_gate": dense((L, D_MODEL, D_FF), D_MODEL),
        "ffn1_w_up": dense((L, D_MODEL, D_FF), D_MODEL),
        "ffn1_w_down": dense((L, D_FF, D_MODEL), D_FF),
        "mix_norm": gain((L, D_MODEL)),
        "w_in": dense((L, D_MODEL, D_IN), D_MODEL),
        "gate_fwd_w": dense((L, GATE_RANK, GLA_KEY_W), GATE_RANK),
        "gate_fwd_b": small((L, GLA_KEY_W), 0.5, 1.0),
        "gate_bwd_w": dense((L, GATE_RANK, GLA_KEY_W), GATE_RANK),
        "gate_bwd_b": small((L, GLA_KEY_W), 0.5, 1.0),
        "gla_norm": gain((L, GLA_VAL_W)),
        "conv_w": dense((L, CONV_K, CONV_W), CONV_K),
        "conv_b": small((L, CONV_W)),
        "w_out": dense((L, D_MIX, D_MODEL), D_MIX),
        "xattn_norm": gain((L, D_MODEL)),
        "mem_norm": gain((L, D_MODEL)),
        "xattn_w_q": dense((L, D_MODEL, D_MODEL), D_MODEL),
        "xattn_w_kv": dense((L, D_MODEL, 2 * D_MODEL), D_MODEL),
        "xattn_w_o": dense((L, D_MODEL, D_MODEL), D_MODEL),
        "ffn2_norm": gain((L, D_MODEL)),
        "ffn2_w_gate": dense((L, D_MODEL, D_FF), D_MODEL),
        "ffn2_w_up": dense((L, D_MODEL, D_FF), D_MODEL),
        "ffn2_w_down": dense((L, D_FF, D_MODEL), D_FF),
        "final_norm": gain((D_MODEL,)),
    }


def reference(x, mem, ffn1_norm, ffn1_w_gate, ffn1_w_up, ffn1_w_down, mix_norm, w_in,
              gate_fwd_w, gate_fwd_b, gate_bwd_w, gate_bwd_b, gla_norm, conv_w, conv_b,
              w_out, xattn_norm, mem_norm, xattn_w_q, xattn_w_kv, xattn_w_o,
              ffn2_norm, ffn2_w_gate, ffn2_w_up, ffn2_w_down, final_norm):
    for l in range(DEPTH):
        x = x + 0.5 * swiglu(rms_norm(x, ffn1_norm[l]), ffn1_w_gate[l], ffn1_w_up[l], ffn1_w_down[l])
        x = x + token_mixing(rms_norm(x, mix_norm[l]), w_in[l], gate_fwd_w[l], gate_fwd_b[l],
                             gate_bwd_w[l], gate_bwd_b[l], gla_norm[l], conv_w[l], conv_b[l], w_out[l])
        x = x + memory_cross_attention(rms_norm(x, xattn_norm[l]), rms_norm(mem, mem_norm[l]),
                                       xattn_w_q[l], xattn_w_kv[l], xattn_w_o[l])
        x = x + 0.5 * swiglu(rms_norm(x, ffn2_norm[l]), ffn2_w_gate[l], ffn2_w_up[l], ffn2_w_down[l])
    return rms_norm(x, final_norm)
```

```python
import numpy as np
import concourse.bass as bass
import concourse.mybir as mybir
from concourse.bass_utils import run_bass_kernel_spmd

F32 = mybir.dt.float32
BF16 = mybir.dt.bfloat16
AF = mybir.ActivationFunctionType
ALU = mybir.AluOpType
AX = mybir.AxisListType

NCORES = 8
T = 2048
D = 1024
KD = 8
DFF = 2816
NF = 22
TT = 512
NTT = 4
EPS = 1e-6
FGROUPS = [(0, 11), (11, 22)]

C_G1 = 0
C_G2 = 8
C_GF = 16
C_GM = 24
C_GX = 32
C_GMEM = 40
NCONST = 64

ENGS = ('pe', 'act', 'dve', 'pool', 'sp')


class Buf:
    __slots__ = ('name', 'w', 'rs', 'excl')

    def __init__(self, name, excl=False):
        self.name = name
        self.w = None
        self.rs = {}
        self.excl = excl


class Sched:
    def __init__(self, nc):
        self.nc = nc
        self.ops = {e: [] for e in ENGS}
        self.cnt = {}
        self.semh = {}
        self.seen = {e: {} for e in ENGS}
        self.nsem = 0

    def add_sem(self, key, handle):
        self.semh[key] = handle
        self.cnt[key] = 0

    def _need(self, eng, ev, waits):
        if ev is None:
            return
        key, val = ev
        if self.seen[eng].get(key, 0) >= val:
            return
        if waits.get(key, 0) < val:
            waits[key] = val

    def _waits(self, eng, reads, writes):
        waits = {}
        for b in reads:
            self._need(eng, b.w, waits)
            if b.excl:
                for k, v in b.rs.items():
                    if k != eng:
                        self._need(eng, (k, v), waits)
        for b in writes:
            self._need(eng, b.w, waits)
            for k, v in b.rs.items():
                self._need(eng, (k, v), waits)
        for key, val in waits.items():
            if key == eng and eng == 'pe':
                continue
            self.seen[eng][key] = val
            semh = self.semh[key]
            self.ops[eng].append(lambda e, semh=semh, val=val: e.wait_ge(semh, val))

    def _mark(self, ev, reads, writes):
        k, v = ev
        for b in reads:
            if b.rs.get(k, 0) < v:
                b.rs[k] = v
        for b in writes:
            b.w = ev
            b.rs = {}

    def op(self, eng, fn, reads=(), writes=()):
        self._waits(eng, reads, writes)
        self.cnt[eng] += 1
        ev = (eng, self.cnt[eng])
        semE = self.semh[eng]
        self.ops[eng].append(lambda e, fn=fn, semE=semE: fn(e).then_inc(semE, 1))
        self._mark(ev, reads, writes)
        return ev

    def op_noinc(self, eng, fn, reads=(), writes=()):
        self._waits(eng, reads, writes)
        self.ops[eng].append(lambda e, fn=fn: fn(e))
        self._mark((eng, self.cnt[eng] + 1), reads, writes)

    def dma(self, q, key, out_ap, in_ap, reads=(), writes=()):
        self._waits(q, reads, writes)
        self.cnt[key] += 16
        ev = (key, self.cnt[key])
        semh = self.semh[key]
        self.ops[q].append(lambda e, o=out_ap, i=in_ap, semh=semh: e.dma_start(out=o, in_=i).then_inc(semh, 16))
        self._mark(ev, reads, writes)
        return ev

    def wait_event(self, eng, ev):
        waits = {}
        self._need(eng, ev, waits)
        for key, val in waits.items():
            self.seen[eng][key] = val
            semh = self.semh[key]
            self.ops[eng].append(lambda e, semh=semh, val=val: e.wait_ge(semh, val))


NGLA = 1056
NPAY = 528
QS = -2.0
SNEG = -1.0 / 16.0
NMEM = 256

C_G1, C_G2, C_GF, C_GM, C_GX, C_GMEM = 0, 8, 16, 24, 32, 40
C_GLAN = 48
C_CW0, C_CW1, C_CW2, C_CB = 52, 56, 60, 64
C_MASK = 68
C_B = 84
C_SELP = 212
C_SELN = 220
NCONST = 232


class Arena:
    def __init__(self, ap, nelem):
        self.ap = ap
        self.n = nelem
        self.off = 0
        self.live = []

    def reset(self, off=0):
        self.off = off

    def alloc(self, shape, dt, bufs):
        n = 1
        for d in shape:
            n *= d
        el = n * (2 if dt == F32 else 1)
        el = (el + 15) // 16 * 16
        a, b = self.off, self.off + el
        assert b <= self.n, ("arena overflow", b, self.n)
        self.off = b
        keep = []
        for (s0, e0, obufs) in self.live:
            if s0 < b and a < e0:
                for ob in obufs:
                    for nb in bufs:
                        if ob.w is not None:
                            k, v = ob.w
                            if nb.rs.get(k, 0) < v:
                                nb.rs[k] = v
                        for k, v in ob.rs.items():
                            if nb.rs.get(k, 0) < v:
                                nb.rs[k] = v
            else:
                keep.append((s0, e0, obufs))
        self.live = [(s0, e0, ob) for (s0, e0, ob) in self.live]
        self.live.append((a, b, list(bufs)))
        v = self.ap[:, a:a + (n * (2 if dt == F32 else 1))]
        if dt == F32:
            v = v.bitcast(F32)
        if len(shape) == 2:
            v = v.rearrange("p (a b) -> p a b", a=shape[0])
        elif len(shape) == 3:
            v = v.rearrange("p (a b c) -> p a b c", a=shape[0], b=shape[1])
        return v


def build_nc(stage=99):
    import os
    DBG_CUT = int(os.environ.get('DBG_CUT', '0'))
    nc = bass.Bass("TRN2", target_bir_lowering=False)
    from contextlib import ExitStack
    es = ExitStack()

    def dram_in(name, shape, dt=F32):
        return nc.dram_tensor(name, list(shape), dt, kind="ExternalInput").ap()

    xT = dram_in("xT", [D, T])
    memT = dram_in("memT", [D, NMEM])
    consts_d = dram_in("consts", [128, NCONST])
    ident_d = dram_in("ident", [128, 128])
    wgu_d = [dram_in("wgu1", [NF, 128, 2 * KD * 128]), dram_in("wgu2", [NF, 128, 2 * KD * 128])]
    wd_d = [dram_in("wd1", [NF, 128, D]), dram_in("wd2", [NF, 128, D])]
    wgla_d = dram_in("wgla", [128, KD * NGLA])
    wr_d = dram_in("wr", [16, 128, KD * 128])
    w2aug_d = dram_in("w2aug", [33, 512])
    tri_d = dram_in("tri", [4, 128, 2048])
    wout_d = dram_in("wout", [128, KD * D])
    wq_d = dram_in("wq", [KD, 128, KD * 128])
    wkv_d = dram_in("wkv", [4, 128, KD * 512])
    wo_d = dram_in("wo", [128, KD * D])
    outT = nc.dram_tensor("outT", [D, T], F32, kind="ExternalOutput").ap()
    pay_in = nc.dram_tensor("pay_in", [128, NPAY], F32)
    pay_out = nc.dram_tensor("pay_out", [NCORES * 128, NPAY], F32)

    def sb(name, shape, dt):
        return es.enter_context(nc.sbuf_tensor(name, list(shape), dt))

    def ps(name, shape, dt=F32):
        return es.enter_context(nc.psum_tensor(name, list(shape), dt))

    S = Sched(nc)
    for e in ENGS:
        S.add_sem(e, es.enter_context(nc.semaphore("s_" + e)))

    def dsem(name):
        if name not in S.semh:
            S.add_sem(name, es.enter_context(nc.semaphore("d_" + name)))
        return name

    x_sb = sb("x_sb", [128, KD, T], F32)
    h_raw = sb("h_raw", [128, KD * T], BF16)
    h_sb = h_raw[:, :].rearrange("p (k t) -> p k t", k=KD)
    sq_raw = sb("sq_raw", [128, 2 * KD * TT], BF16)
    rstd_sb = [sb(f"rstd_sb{i}", [128, TT], F32) for i in range(2)]
    consts = sb("consts_sb", [128, NCONST], F32)
    ones_bf = sb("ones_bf", [128, 128], BF16)
    ident_bf = sb("ident_bf", [128, 128], BF16)
    ARENA_N = 46208
    ar_raw = sb("arena", [128, ARENA_N], BF16)
    AR = Arena(ar_raw, ARENA_N)
    SQA = Arena(sq_raw, 2 * KD * TT)
    HA = Arena(h_raw, KD * T)

    psum = [ps(f"psum{i}", [128, TT]) for i in range(8)]
    PS = [Buf(f"ps{i}", excl=True) for i in range(8)]

    X = [[Buf(f"x{k}_{t}") for t in range(NTT)] for k in range(KD)]
    H = [Buf(f"h{t}") for t in range(NTT)]
    HA.live.append((0, KD * T, H))
    RSTD = [Buf(f"rstd{i}") for i in range(2)]
    CONSTS = Buf("consts")
    ONES = Buf("ones")
    IDENT = Buf("ident")

    tsl = lambda t: slice(t * TT, (t + 1) * TT)
    cnt = {'norm': 0, 'gu': 0, 'dn': 0, 'wgu': 0}

    S.dma('sp', dsem("consts"), consts[:, :], consts_d[:, :], writes=[CONSTS])
    xT_v = xT.rearrange("(k p) t -> p k t", p=128)
    for t in range(NTT):
        S.dma('sp', dsem(f"xin{t}"), x_sb[:, :, tsl(t)], xT_v[:, :, tsl(t)], writes=[X[k][t] for k in range(KD)])
    S.op('pool', lambda e: e.memset(ones_bf[:, :], 1.0), writes=[ONES])
    S.dma('pool', dsem("ident"), ident_bf[:, :], ident_d[:, :], writes=[IDENT])

    def mm_group(out_ap, pairs, reads, writes):
        n = len(pairs)
        for i, (l, r) in enumerate(pairs):
            fn = (lambda e, l=l, r=r, i=i: e.matmul(out_ap, lhsT=l, rhs=r, start=(i == 0), stop=(i == n - 1)))
            if i < n - 1:
                S.op_noinc('pe', fn, reads=reads, writes=writes)
            else:
                S.op('pe', fn, reads=reads, writes=writes)

    def emit_rstd_generic(sq_ap_fn, ncols, pb, src_bufs, src_ap, scale, nk=KD):
        i = cnt['norm'] % 2
        cnt['norm'] += 1
        SQA.reset(i * KD * TT)
        sqb = Buf("sq")
        sq = SQA.alloc([nk, ncols], BF16, [sqb])
        S.op('pool', lambda e: e.tensor_tensor(out=sq, in0=src_ap, in1=src_ap, op=ALU.mult), reads=src_bufs, writes=[sqb])
        mm_group(psum[pb][:, 0:ncols], [(ones_bf[:, :], sq[:, k, :]) for k in range(nk)], [sqb, ONES], [PS[pb]])
        S.op('act', lambda e: e.activation(out=rstd_sb[i][:, 0:ncols], in_=psum[pb][:, 0:ncols], func=AF.Ln, bias=EPS, scale=scale),
             reads=[PS[pb]], writes=[RSTD[i]])
        S.op('act', lambda e: e.activation(out=rstd_sb[i][:, 0:ncols], in_=rstd_sb[i][:, 0:ncols], func=AF.Exp, scale=-0.5),
             reads=[RSTD[i]], writes=[RSTD[i]])
        return i

    def emit_norm_to_h(t, gcol):
        i = emit_rstd_generic(None, TT, 6 + (cnt['norm'] % 2), [X[k][t] for k in range(KD)], x_sb[:, :, tsl(t)], 1.0 / D)
        for k in range(KD):
            S.op('dve', lambda e, k=k: e.scalar_tensor_tensor(out=h_sb[:, k, tsl(t)], in0=x_sb[:, k, tsl(t)],
                                                               scalar=consts[:, gcol + k:gcol + k + 1], in1=rstd_sb[i][:, :],
                                                               op0=ALU.mult, op1=ALU.mult),
                 reads=[X[k][t], RSTD[i], CONSTS], writes=[H[t]])

    def emit_ffn(li, gcol):
        AR.reset()
        ACTB = [[Buf(f"act{f}_{t}") for t in range(NTT)] for f in range(11)]
        WD = [Buf(f"wd{f}") for f in range(11)]
        NWGU = 3
        WGU = [Buf(f"wgu{i}") for i in range(NWGU)]
        SILU = [Buf(f"silu{i}") for i in range(2)]
        act_sb = AR.alloc([11, T], BF16, [b for row in ACTB for b in row])
        wd_sb = AR.alloc([11, D], BF16, WD)
        wgu_sb = [AR.alloc([2, KD, 128], BF16, [WGU[i]]) for i in range(NWGU)]
        silu_sb = [AR.alloc([TT], F32, [SILU[i]]) for i in range(2)]
        for i in range(NWGU):
            dsem(f"wgu{i}")
        for f in range(11):
            dsem(f"wd{f}")

        def load_wgu(f):
            i = cnt['wgu'] % NWGU
            cnt['wgu'] += 1
            S.dma('pool', f"wgu{i}", wgu_sb[i].rearrange("p a k n -> p (a k n)"), wgu_d[li][f, :, :], writes=[WGU[i]])
            return i

        for t in range(NTT):
            emit_norm_to_h(t, gcol)
        for (f0, f1) in FGROUPS:
            nf = f1 - f0
            slots = {}
            pre = min(NWGU - 1, nf)
            for j in range(pre):
                slots[f0 + j] = load_wgu(f0 + j)
            for fl in range(nf):
                f = f0 + fl
                if fl + pre < nf:
                    slots[f + pre] = load_wgu(f + pre)
                S.dma('pool', f"wd{fl}", wd_sb[:, fl, :], wd_d[li][f, :, :], writes=[WD[fl]])
                ws = slots[f]
                for t in range(NTT):
                    j = cnt['gu'] % 2
                    cnt['gu'] += 1
                    for a, pb in ((0, j), (1, 2 + j)):
                        mm_group(psum[pb][:, :], [(wgu_sb[ws][:, a, k, :], h_sb[:, k, tsl(t)]) for k in range(KD)],
                                 [WGU[ws], H[t]], [PS[pb]])
                    S.op('act', lambda e, j=j: e.activation(out=silu_sb[j], in_=psum[j][:, :], func=AF.Silu),
                         reads=[PS[j]], writes=[SILU[j]])
                    S.op('dve', lambda e, j=j, fl=fl, t=t: e.tensor_tensor(out=act_sb[:, fl, tsl(t)], in0=silu_sb[j],
                                                                          in1=psum[2 + j][:, :], op=ALU.mult),
                         reads=[SILU[j], PS[2 + j]], writes=[ACTB[fl][t]])
            for t in range(NTT):
                for dc in range(KD):
                    j = cnt['dn'] % 2
                    cnt['dn'] += 1
                    pb = 4 + j
                    mm_group(psum[pb][:, :], [(wd_sb[:, fl, dc * 128:(dc + 1) * 128], act_sb[:, fl, tsl(t)]) for fl in range(nf)],
                             [WD[fl] for fl in range(nf)] + [ACTB[fl][t] for fl in range(nf)], [PS[pb]])
                    S.op('dve', lambda e, dc=dc, pb=pb, t=t: e.scalar_tensor_tensor(
                        out=x_sb[:, dc, tsl(t)], in0=psum[pb][:, :], scalar=0.5, in1=x_sb[:, dc, tsl(t)],
                        op0=ALU.mult, op1=ALU.add), reads=[PS[pb], X[dc][t]], writes=[X[dc][t]])

    def emit_mixer():
        for t in range(NTT):
            emit_norm_to_h(t, C_GM)
        AR.reset()
        SQA.reset()
        O = [[Buf(f"o{h}_{t}") for t in range(NTT)] for h in range(4)]
        QQ = [[Buf(f"qq{d}_{t}") for t in range(NTT)] for d in range(2)]
        WGLAK = [Buf(f"wgla{k}") for k in range(KD)]
        o_sb = AR.alloc([4, T], F32, [b for row in O for b in row])
        qq_sb = [AR.alloc([2, T], BF16, QQ[d]) for d in range(2)]
        mark_after_qq = AR.off
        wgla = AR.alloc([KD, NGLA], BF16, WGLAK)
        TRIB, REVB = Buf("tri"), Buf("rev")
        tri_sb = AR.alloc([4, TT], BF16, [TRIB])
        rev_sb = AR.alloc([4, TT], BF16, [REVB])
        LTM, QIN, KIN, KST, VSB, ATT = Buf("ltm"), Buf("qin"), Buf("kin"), Buf("kst"), Buf("vsb"), [Buf(f"att{i}") for i in range(4)]
        l_tm = AR.alloc([4, 256], BF16, [LTM])
        qin = AR.alloc([2, TT], BF16, [QIN])
        kin = AR.alloc([2, TT], BF16, [KIN])
        kst = AR.alloc([4, 256], BF16, [KST])
        v_sb = AR.alloc([4, 512], BF16, [VSB])
        att_sb = AR.alloc([4, TT], BF16, ATT)
        PAY = Buf("pay")
        pay_sb = SQA.alloc([NPAY], F32, [PAY])
        E = [Buf("E0"), Buf("E1")]
        e_sb = [SQA.alloc([TT], F32, [E[i]]) for i in range(2)]
        ZAUG, W2, ST, STB, SMALL = Buf("zaug"), Buf("w2"), [Buf("S0"), Buf("S1")], [Buf("Sb0"), Buf("Sb1")], Buf("small")
        zaug = SQA.alloc([2, TT], BF16, [ZAUG])
        w2aug = SQA.alloc([512], F32, [W2])
        W2S = Buf("w2s")
        w2s = SQA.alloc([2, 512], BF16, [W2S])
        s_sb = [SQA.alloc([256], F32, [ST[j]]) for j in range(2)]
        sbf_sb = [SQA.alloc([256], BF16, [STB[j]]) for j in range(2)]
        small = SQA.alloc([32], F32, [SMALL])

        for k in range(KD):
            S.dma('pool', dsem(f"wgla{k}"), wgla[:, k, :], wgla_d[:, k * NGLA:(k + 1) * NGLA], writes=[WGLAK[k]])
        S.dma('sp', dsem("w2"), w2aug[0:33, :], w2aug_d[:, :], writes=[W2])
        S.op('dve', lambda e: e.memset(zaug[:, 0, :], 1.0), writes=[ZAUG])
        S.op('dve', lambda e: e.memset(zaug[:, 1, :], 0.0), writes=[ZAUG])
        S.op('dve', lambda e: e.tensor_copy(out=w2s[0:33, 0, :], in_=w2aug[0:33, :]), reads=[W2], writes=[W2S])
        S.op('dve', lambda e: e.tensor_tensor(out=w2s[0:33, 1, :], in0=w2aug[0:33, :], in1=w2s[0:33, 0, :], op=ALU.subtract),
             reads=[W2, W2S], writes=[W2S])
        S.op('dve', lambda e: e.memset(small[:, :], 0.0), writes=[SMALL])
        S.op('dve', lambda e: e.memset(pay_sb[:, :], 0.0), writes=[PAY])
        dsem("tri"); dsem("rev")

        ecnt = [0]

        def E_alloc():
            i = ecnt[0] % 2
            ecnt[0] += 1
            return i

        def tile_dir(t, dr, first):
            tk = tsl(t)
            hb = lambda b: slice(t * TT + b * 128, t * TT + (b + 1) * 128)
            mm_group(psum[0][0:32, :], [(wgla[:, k, 1024:1056], h_sb[:, k, tk]) for k in range(KD)], WGLAK + [H[t]], [PS[0]])
            S.op('dve', lambda e: e.tensor_copy(out=zaug[0:32, 0, :], in_=psum[0][0:32, :]), reads=[PS[0]], writes=[ZAUG])
            S.op('dve', lambda e: e.tensor_tensor(out=zaug[0:32, 1, :], in0=psum[0][0:32, :], in1=zaug[0:32, 0, :], op=ALU.subtract),
                 reads=[PS[0], ZAUG], writes=[ZAUG])
            if DBG_CUT == 1:
                return
            for half in range(2):
                pb = 1 + half
                for bb in range(2):
                    b = half * 2 + bb
                    csl = slice(dr * 256, (dr + 1) * 256)
                    bsl = slice(b * 128, (b + 1) * 128)
                    mm_group(psum[pb][:, bb * 256:(bb + 1) * 256],
                             [(zaug[0:33, 0, bsl], w2s[0:33, 0, csl]), (zaug[0:33, 1, bsl], w2s[0:33, 0, csl]), (zaug[0:33, 0, bsl], w2s[0:33, 1, csl])],
                             [ZAUG, W2S], [PS[pb]])
                i = E_alloc()
                S.op('act', lambda e, pb=pb, i=i: e.activation(out=e_sb[i], in_=psum[pb][:, :], func=AF.Exp, scale=-1.0),
                     reads=[PS[pb]], writes=[E[i]])
                S.op('act', lambda e, i=i, half=half: e.activation(out=l_tm[:, 2 * half:2 * half + 2, :].rearrange("p a b -> p (a b)"),
                                                                  in_=e_sb[i], func=AF.Ln, bias=1.0, scale=1.0),
                     reads=[E[i]], writes=[LTM])
            if DBG_CUT == 2:
                return
            for j in range(2):
                mm_group(psum[3 + j][:, :], [(l_tm[:, b, j * 128:(j + 1) * 128], tri_sb[:, b, :]) for b in range(4)],
                         [LTM, TRIB], [PS[3 + j]])
            endcol = TT - 1 if dr == 0 else 0
            for j in range(2):
                S.op('dve', lambda e, j=j: e.tensor_copy(out=small[:, j:j + 1], in_=psum[3 + j][:, endcol:endcol + 1]),
                     reads=[PS[3 + j]], writes=[SMALL])
            if DBG_CUT == 3:
                return
            for j in range(2):
                i1 = E_alloc()
                S.op('act', lambda e, j=j, i1=i1: e.activation(out=e_sb[i1], in_=psum[3 + j][:, :], func=AF.Exp, scale=1.0),
                     reads=[PS[3 + j]], writes=[E[i1]])
                if DBG_CUT == 41:
                    continue
                mm_group(psum[5][:, :], [(wgla[:, k, j * 128:(j + 1) * 128], h_sb[:, k, tk]) for k in range(KD)], WGLAK + [H[t]], [PS[5]])
                if DBG_CUT == 42:
                    continue
                S.op('dve', lambda e, j=j, i1=i1: e.scalar_tensor_tensor(out=qin[:, j, :], in0=psum[5][:, :], scalar=QS, in1=e_sb[i1],
                                                                       op0=ALU.mult, op1=ALU.mult),
                     reads=[PS[5], E[i1]], writes=[QIN])
                if DBG_CUT == 43:
                    continue
                i2 = E_alloc()
                S.op('act', lambda e, j=j, i2=i2: e.activation(out=e_sb[i2], in_=psum[3 + j][:, :], func=AF.Exp, scale=-1.0),
                     reads=[PS[3 + j]], writes=[E[i2]])
                mm_group(psum[6][:, :], [(wgla[:, k, 256 + j * 128:256 + (j + 1) * 128], h_sb[:, k, tk]) for k in range(KD)],
                         WGLAK + [H[t]], [PS[6]])
                S.op('dve', lambda e, j=j, i2=i2: e.tensor_tensor(out=kin[:, j, :], in0=psum[6][:, :], in1=e_sb[i2], op=ALU.mult),
                     reads=[PS[6], E[i2]], writes=[KIN])
            if DBG_CUT in (4, 41, 42, 43):
                return
            S.op('act', lambda e: e.activation(out=small[:, 4:6], in_=small[:, 8 + dr * 2:10 + dr * 2], func=AF.Exp),
                 reads=[SMALL], writes=[SMALL])
            for j in range(2):
                S.op('dve', lambda e, j=j: e.tensor_scalar(out=qq_sb[dr][:, j, tk], in0=qin[:, j, :], scalar1=small[:, 4 + j:5 + j],
                                                           scalar2=None, op0=ALU.mult),
                     reads=[QIN, SMALL], writes=[QQ[dr][t]])
            S.op('dve', lambda e: e.tensor_tensor(out=small[:, 8 + dr * 2:10 + dr * 2], in0=small[:, 8 + dr * 2:10 + dr * 2],
                                                  in1=small[:, 0:2], op=ALU.add), reads=[SMALL], writes=[SMALL])
            S.op('act', lambda e: e.activation(out=small[:, 2:4], in_=small[:, 0:2], func=AF.Exp), reads=[SMALL], writes=[SMALL])
            if DBG_CUT == 5:
                return
            for half in range(2):
                pb = 0 + half
                for bb in range(2):
                    b = half * 2 + bb
                    srcs = [bp for bp in range(4) if (bp >= b if dr == 0 else bp <= b)]
                    mm_group(psum[pb][:, bb * 256:(bb + 1) * 256],
                             [(rev_sb[:, bp, b * 128:(b + 1) * 128], l_tm[:, bp, :]) for bp in srcs], [REVB, LTM], [PS[pb]])
                i = E_alloc()
                S.op('act', lambda e, pb=pb, i=i: e.activation(out=e_sb[i], in_=psum[pb][:, :], func=AF.Exp), reads=[PS[pb]], writes=[E[i]])
                pk = 2 + half if half == 0 else 7
                for bb in range(2):
                    b = half * 2 + bb
                    mm_group(psum[pk][:, bb * 256:(bb + 1) * 256], [(h_sb[:, k, hb(b)], wgla[:, k, 256:512]) for k in range(KD)],
                             WGLAK + [H[t]], [PS[pk]])
                S.op('dve', lambda e, pk=pk, i=i, half=half: e.tensor_tensor(
                    out=kst[:, 2 * half:2 * half + 2, :].rearrange("p a b -> p (a b)"), in0=psum[pk][:, :], in1=e_sb[i], op=ALU.mult),
                    reads=[PS[pk], E[i]], writes=[KST])
            if DBG_CUT == 6:
                return
            for b in range(4):
                pb = [5, 6, 3, 4][b]
                mm_group(psum[pb][:, :], [(h_sb[:, k, hb(b)], wgla[:, k, 512:1024]) for k in range(KD)], WGLAK + [H[t]], [PS[pb]])
                if b % 2 == 0:
                    S.op('act', lambda e, b=b, pb=pb: e.copy(out=v_sb[:, b, :], in_=psum[pb][:, :]), reads=[PS[pb]], writes=[VSB])
                else:
                    S.op('dve', lambda e, b=b, pb=pb: e.tensor_copy(out=v_sb[:, b, :], in_=psum[pb][:, :]), reads=[PS[pb]], writes=[VSB])
            if DBG_CUT == 7:
                return
            for h in range(4):
                j, r0 = h // 2, (h % 2) * 64
                rows = slice(r0, r0 + 64)
                order = [0, 1, 2, 3] if dr == 0 else [3, 2, 1, 0]
                cr = {}
                for bm in order:
                    cr[bm] = slice(bm * 128, TT) if dr == 0 else slice(0, (bm + 1) * 128)
                    pb = bm % 2
                    S.op('pe', lambda e, bm=bm, pb=pb, rows=rows, j=j, c=cr[bm]: e.matmul(
                        psum[pb][:, c], lhsT=kin[rows, j, bm * 128:(bm + 1) * 128], rhs=qin[rows, j, c], start=True, stop=True),
                        reads=[KIN, QIN], writes=[PS[pb]])
                    S.op('dve', lambda e, bm=bm, pb=pb, c=cr[bm]: e.tensor_tensor(out=att_sb[:, bm, c], in0=psum[pb][:, c],
                                                                                 in1=tri_sb[:, bm, c], op=ALU.mult),
                         reads=[PS[pb], TRIB], writes=[ATT[bm]])
                po = 2 + (h % 2)
                pairs = []
                rd = [VSB] + ATT
                if not first:
                    pairs.append((sbf_sb[j][rows, (h % 2) * 128:(h % 2) * 128 + 128], qin[rows, j, :], slice(0, TT)))
                    rd = rd + [STB[j], QIN]
                for bm in order:
                    pairs.append((v_sb[:, bm, h * 128:(h + 1) * 128], att_sb[:, bm, cr[bm]], cr[bm]))
                n = len(pairs)
                for i, (l, r, c) in enumerate(pairs):
                    fn = (lambda e, l=l, r=r, c=c, i=i, po=po: e.matmul(psum[po][:, c], lhsT=l, rhs=r, start=(i == 0), stop=(i == n - 1)))
                    if i < n - 1:
                        S.op_noinc('pe', fn, reads=rd, writes=[PS[po]])
                    else:
                        S.op('pe', fn, reads=rd, writes=[PS[po]])
                if dr == 0:
                    S.op('act', lambda e, h=h, po=po: e.copy(out=o_sb[:, h, tk], in_=psum[po][:, :]), reads=[PS[po]], writes=[O[h][t]])
                else:
                    S.op('dve', lambda e, h=h, po=po: e.tensor_tensor(out=o_sb[:, h, tk], in0=o_sb[:, h, tk], in1=psum[po][:, :], op=ALU.add),
                         reads=[PS[po], O[h][t]], writes=[O[h][t]])
            if DBG_CUT == 8:
                return
            for j in range(2):
                pb = 4 + j
                mm_group(psum[pb][:, 0:256], [(kst[:, b, j * 128:(j + 1) * 128], v_sb[:, b, j * 256:(j + 1) * 256]) for b in range(4)],
                         [KST, VSB], [PS[pb]])
                if first:
                    S.op('dve', lambda e, j=j, pb=pb: e.tensor_copy(out=s_sb[j], in_=psum[pb][:, 0:256]), reads=[PS[pb]], writes=[ST[j]])
                else:
                    S.op('dve', lambda e, j=j, pb=pb: e.scalar_tensor_tensor(out=s_sb[j], in0=s_sb[j], scalar=small[:, 2 + j:3 + j],
                                                                           in1=psum[pb][:, 0:256], op0=ALU.mult, op1=ALU.add),
                         reads=[PS[pb], ST[j], SMALL], writes=[ST[j]])
                S.op('act', lambda e, j=j: e.activation(out=sbf_sb[j], in_=s_sb[j], func=AF.Identity, scale=SNEG), reads=[ST[j]], writes=[STB[j]])

        for dr in range(2):
            S.dma('pool', "tri", tri_sb.rearrange("p a b -> p (a b)"), tri_d[2 * dr, :, :], writes=[TRIB])
            S.dma('pool', "rev", rev_sb.rearrange("p a b -> p (a b)"), tri_d[2 * dr + 1, :, :], writes=[REVB])
            tiles = [0, 1, 2, 3] if dr == 0 else [3, 2, 1, 0]
            for n_, t in enumerate(tiles):
                tile_dir(t, dr, n_ == 0)
            for j in range(2):
                for hh in range(2):
                    rows = slice(hh * 64, hh * 64 + 64)
                    S.op('dve', lambda e, j=j, hh=hh, rows=rows, dr=dr: e.tensor_copy(
                        out=pay_sb[rows, (dr * 2 + j) * 128:(dr * 2 + j + 1) * 128], in_=s_sb[j][rows, hh * 128:(hh + 1) * 128]),
                        reads=[ST[j]], writes=[PAY])
        S.op('dve', lambda e: e.tensor_copy(out=pay_sb[:, 512:516], in_=small[:, 8:12]), reads=[SMALL], writes=[PAY])
        return dict(o_sb=o_sb, qq_sb=qq_sb, O=O, QQ=QQ, pay_sb=pay_sb, PAY=PAY, mark=mark_after_qq)

    def emit_pass_r(M):
        AR.reset(M['mark'])
        RS = [[Buf(f"rs{h}_{t}") for t in range(NTT)] for h in range(4)]
        CO = [[Buf(f"co{c}_{t}") for t in range(NTT)] for c in range(4)]
        rs_sb = AR.alloc([4, T], BF16, [b for row in RS for b in row])
        co_sb = AR.alloc([4, T], BF16, [b for row in CO for b in row])
        BND = Buf("bnd")
        bnd = AR.alloc([4, NTT, 6], F32, [BND])
        M['mark2'] = AR.off
        NW = 4
        WR = [Buf(f"wr{i}") for i in range(NW)]
        wr_sb = [AR.alloc([KD, 128], BF16, [WR[i]]) for i in range(NW)]
        U = [Buf("u0"), Buf("u1")]
        CV = [Buf("cv0"), Buf("cv1")]
        SQA.reset(1056)
        u_sb = [SQA.alloc([TT], F32, [U[i]]) for i in range(2)]
        cv_sb = [SQA.alloc([TT], F32, [CV[i]]) for i in range(2)]
        for i in range(NW):
            dsem(f"wr{i}")
        wcnt = [0]

        def load_wr(idx):
            i = wcnt[0] % NW
            wcnt[0] += 1
            S.dma('pool', f"wr{i}", wr_sb[i].rearrange("p k n -> p (k n)"), wr_d[idx, :, :], writes=[WR[i]])
            return i

        n = 0
        for c in range(4):
            sl = [load_wr(c), load_wr(4 + c), load_wr(8 + c)]
            for t in range(NTT):
                i = n % 2
                n += 1
                tk = tsl(t)
                for a, pb in ((0, 0 + i), (1, 2 + i), (2, 4 + i)):
                    mm_group(psum[pb][:, :], [(wr_sb[sl[a]][:, k, :], h_sb[:, k, tk]) for k in range(KD)], [WR[sl[a]], H[t]], [PS[pb]])
                S.op('act', lambda e, i=i: e.copy(out=cv_sb[i], in_=psum[2 + i][:, :]), reads=[PS[2 + i]], writes=[CV[i]])
                S.op('dve', lambda e, i=i: e.tensor_tensor(out=u_sb[i], in0=cv_sb[i], in1=psum[4 + i][:, :], op=ALU.mult),
                     reads=[CV[i], PS[4 + i]], writes=[U[i]])
                S.op('act', lambda e, i=i, c=c: e.activation(out=cv_sb[i], in_=u_sb[i], func=AF.Identity,
                                                            bias=consts[:, C_CB + c:C_CB + c + 1], scale=consts[:, C_CW1 + c:C_CW1 + c + 1]),
                     reads=[U[i], CONSTS], writes=[CV[i]])
                S.op('dve', lambda e, i=i, c=c: e.scalar_tensor_tensor(out=cv_sb[i][:, 1:TT], in0=u_sb[i][:, 0:TT - 1],
                                                                      scalar=consts[:, C_CW0 + c:C_CW0 + c + 1], in1=cv_sb[i][:, 1:TT],
                                                                      op0=ALU.mult, op1=ALU.add), reads=[U[i], CV[i], CONSTS], writes=[CV[i]])
                S.op('dve', lambda e, i=i, c=c: e.scalar_tensor_tensor(out=cv_sb[i][:, 0:TT - 1], in0=u_sb[i][:, 1:TT],
                                                                      scalar=consts[:, C_CW2 + c:C_CW2 + c + 1], in1=cv_sb[i][:, 0:TT - 1],
                                                                      op0=ALU.mult, op1=ALU.add), reads=[U[i], CV[i], CONSTS], writes=[CV[i]])
                S.op('dve', lambda e, i=i, c=c, tk=tk: e.tensor_tensor(out=co_sb[:, c, tk], in0=cv_sb[i], in1=psum[0 + i][:, :], op=ALU.mult),
                     reads=[CV[i], PS[0 + i]], writes=[CO[c][t]])
                for kind, src, col, rb in ((0, u_sb[i], 0, U[i]), (1, u_sb[i], TT - 1, U[i]), (4, cv_sb[i], 0, CV[i]), (5, cv_sb[i], TT - 1, CV[i])):
                    S.op('dve', lambda e, kind=kind, src=src, col=col, c=c, t=t: e.tensor_copy(out=bnd[:, c, t, kind:kind + 1], in_=src[:, col:col + 1]),
                         reads=[rb], writes=[BND])
                for kind, col in ((2, 0), (3, TT - 1)):
                    S.op('dve', lambda e, kind=kind, col=col, c=c, t=t, i=i: e.tensor_copy(out=bnd[:, c, t, kind:kind + 1], in_=psum[0 + i][:, col:col + 1]),
                         reads=[PS[0 + i]], writes=[BND])
        for h in range(4):
            s0 = load_wr(12 + h)
            for t in range(NTT):
                i = n % 2
                n += 1
                pb = 6 + i
                mm_group(psum[pb][:, :], [(wr_sb[s0][:, k, :], h_sb[:, k, tsl(t)]) for k in range(KD)], [WR[s0], H[t]], [PS[pb]])
                S.op('act', lambda e, h=h, t=t, pb=pb: e.activation(out=rs_sb[:, h, tsl(t)], in_=psum[pb][:, :], func=AF.Silu),
                     reads=[PS[pb]], writes=[RS[h][t]])
        M.update(rs_sb=rs_sb, co_sb=co_sb, RS=RS, CO=CO, bnd=bnd, BND=BND)

    def emit_collective(M):
        pay_sb, PAY, bnd, BND = M['pay_sb'], M['PAY'], M['bnd'], M['BND']
        S.op('dve', lambda e: e.tensor_copy(out=pay_sb[:, 516:520], in_=bnd[:, :, 0, 0]), reads=[BND], writes=[PAY])
        S.op('dve', lambda e: e.tensor_copy(out=pay_sb[:, 520:524], in_=bnd[:, :, NTT - 1, 1]), reads=[BND], writes=[PAY])
        PAYD, GOUT = Buf("payd"), Buf("gout")
        S.dma('pool', dsem("payd"), pay_in.ap()[:, :], pay_sb[:, :], reads=[PAY], writes=[PAYD])
        ccs = es.enter_context(nc.semaphore("cc_sem"))
        S.add_sem("cc", ccs)
        S._waits('pool', [PAYD], [GOUT])
        S.cnt["cc"] += 1
        ev = ("cc", S.cnt["cc"])
        S.ops['pool'].append(lambda e: e.collective_compute("AllGather", ALU.bypass, replica_groups=[list(range(NCORES))],
                                                           ins=[pay_in.ap().opt()], outs=[pay_out.ap().opt()]).then_inc(ccs))
        S._mark(ev, [PAYD], [GOUT])
        M.update(GOUT=GOUT)

    def emit_mixer_finish(M):
        o_sb, qq_sb, O, QQ = M['o_sb'], M['qq_sb'], M['O'], M['QQ']
        rs_sb, co_sb, RS, CO, bnd, BND = M['rs_sb'], M['co_sb'], M['RS'], M['CO'], M['bnd'], M['BND']
        HA.reset()
        G, WOUT_, SIN, SINB, CMB = Buf("G"), Buf("wout"), [Buf(f"sin{i}") for i in range(4)], [Buf(f"sinb{i}") for i in range(4)], Buf("cmb")
        WOUT2 = Buf("wout2")
        g_sb = HA.alloc([NCORES, NPAY], F32, [G])
        WOUTK = [Buf(f"woutk{k}") for k in range(8)]
        wout_a = HA.alloc([4, D], BF16, WOUTK[0:4])
        AR.reset(M['mark2'])
        sin_sb = [AR.alloc([128], F32, [SIN[i]]) for i in range(4)]
        sinb_sb = [AR.alloc([256], BF16, [SINB[i]]) for i in range(4)]
        cmb = AR.alloc([512], F32, [CMB])
        S.dma('sp', dsem("gload"), g_sb, pay_out.ap().rearrange("(r p) n -> p r n", p=128), reads=[M['GOUT']], writes=[G])
        for k in range(4):
            S.dma('pool', dsem(f"wout{k}"), wout_a[:, k, :], wout_d[:, k * D:(k + 1) * D], writes=[WOUTK[k]])
        for i in range(4):
            S.op('pool', lambda e, i=i: e.memset(sinb_sb[i], 0.0), writes=[SINB[i]])
        for dr in range(2):
            Bv = consts[:, C_B + dr * 64:C_B + (dr + 1) * 64].rearrange("p (r m) -> p r m", r=8)
            Lv = g_sb[:, :, 512 + dr * 2:514 + dr * 2].rearrange("p m j -> p j m")
            tmp = cmb[:, 0:128].rearrange("p (r j m) -> p r j m", r=8, j=2)
            S.op('dve', lambda e, Bv=Bv, Lv=Lv, tmp=tmp: e.tensor_tensor(out=tmp, in0=Bv.unsqueeze(2).to_broadcast([128, 8, 2, 8]),
                                                                        in1=Lv.unsqueeze(1).to_broadcast([128, 8, 2, 8]), op=ALU.mult),
                 reads=[G, CONSTS], writes=[CMB])
            Ev = cmb[:, 128:144].rearrange("p (r j) -> p r j", r=8)
            S.op('dve', lambda e, tmp=tmp, Ev=Ev: e.tensor_reduce(out=Ev, in_=tmp, axis=AX.X, op=ALU.add), reads=[CMB], writes=[CMB])
            Wv = cmb[:, 160:176].rearrange("p (r j) -> p r j", r=8)
            S.op('act', lambda e, Ev=Ev, Wv=Wv: e.activation(out=Wv, in_=Ev, func=AF.Exp), reads=[CMB], writes=[CMB])
            Mv = consts[:, C_MASK + dr * 8:C_MASK + (dr + 1) * 8]
            S.op('dve', lambda e, Wv=Wv, Mv=Mv: e.tensor_tensor(out=Wv, in0=Wv, in1=Mv.unsqueeze(2).to_broadcast([128, 8, 2]), op=ALU.mult),
                 reads=[CMB, CONSTS], writes=[CMB])
            for j in range(2):
                i = dr * 2 + j
                for r in range(NCORES):
                    src = g_sb[:, r, i * 128:(i + 1) * 128]
                    wcol = cmb[:, 160 + r * 2 + j:161 + r * 2 + j]
                    if r == 0:
                        S.op('dve', lambda e, i=i, src=src, wcol=wcol: e.tensor_scalar(out=sin_sb[i], in0=src, scalar1=wcol, scalar2=None, op0=ALU.mult),
                             reads=[G, CMB], writes=[SIN[i]])
                    else:
                        S.op('dve', lambda e, i=i, src=src, wcol=wcol: e.scalar_tensor_tensor(out=sin_sb[i], in0=src, scalar=wcol, in1=sin_sb[i],
                                                                                            op0=ALU.mult, op1=ALU.add),
                             reads=[G, CMB, SIN[i]], writes=[SIN[i]])
                for hh in range(2):
                    rows = slice(hh * 64, hh * 64 + 64)
                    S.op('act', lambda e, i=i, rows=rows, hh=hh: e.activation(out=sinb_sb[i][rows, hh * 128:(hh + 1) * 128], in_=sin_sb[i][rows, :],
                                                                             func=AF.Identity, scale=SNEG), reads=[SIN[i]], writes=[SINB[i]])
        hp, hn = cmb[:, 192:196], cmb[:, 196:200]
        for r in range(NCORES):
            for (dst, c0, selc) in ((hp, 520, C_SELP), (hn, 516, C_SELN)):
                src = g_sb[:, r, c0:c0 + 4]
                sc = consts[:, selc + r:selc + r + 1]
                if r == 0:
                    S.op('dve', lambda e, dst=dst, src=src, sc=sc: e.tensor_scalar(out=dst, in0=src, scalar1=sc, scalar2=None, op0=ALU.mult),
                         reads=[G, CONSTS], writes=[CMB])
                else:
                    S.op('dve', lambda e, dst=dst, src=src, sc=sc: e.scalar_tensor_tensor(out=dst, in0=src, scalar=sc, in1=dst, op0=ALU.mult, op1=ALU.add),
                         reads=[G, CONSTS, CMB], writes=[CMB])
        for c in range(4):
            for t in range(NTT):
                prev = hp[:, c:c + 1] if t == 0 else bnd[:, c, t - 1, 1:2]
                nxt = hn[:, c:c + 1] if t == NTT - 1 else bnd[:, c, t + 1, 0:1]
                for (nb, wc, cvk, bgk, col) in ((prev, C_CW0, 4, 2, t * TT), (nxt, C_CW2, 5, 3, t * TT + TT - 1)):
                    tmpc = cmb[:, 200:201]
                    S.op('dve', lambda e, nb=nb, wc=wc, cvk=cvk, c=c, t=t, tmpc=tmpc: e.scalar_tensor_tensor(
                        out=tmpc, in0=nb, scalar=consts[:, wc + c:wc + c + 1], in1=bnd[:, c, t, cvk:cvk + 1], op0=ALU.mult, op1=ALU.add),
                        reads=[CMB, BND, CONSTS], writes=[CMB])
                    S.op('dve', lambda e, tmpc=tmpc, bgk=bgk, c=c, t=t, col=col: e.tensor_tensor(
                        out=co_sb[:, c, col:col + 1], in0=tmpc, in1=bnd[:, c, t, bgk:bgk + 1], op=ALU.mult),
                        reads=[CMB, BND], writes=[CO[c][t]])
        SQA.reset()
        OSQ = [Buf("osq0"), Buf("osq1")]
        osq = [SQA.alloc([TT], BF16, [OSQ[i]]) for i in range(2)]
        OT = [Buf("ot0"), Buf("ot1")]
        ot = [SQA.alloc([TT], F32, [OT[i]]) for i in range(2)]
        wout_b = SQA.alloc([4, D], BF16, WOUTK[4:8])
        for k in range(4):
            S.dma('pool', dsem(f"wout{4 + k}"), wout_b[:, k, :], wout_d[:, (4 + k) * D:(5 + k) * D], writes=[WOUTK[4 + k]])
        n = 0
        for t in range(NTT):
            for h in range(4):
                i = n % 2
                n += 1
                j, rows, tk = h // 2, slice((h % 2) * 64, (h % 2) * 64 + 64), tsl(t)
                pb = 0 + i
                mm_group(psum[pb][:, :], [(sinb_sb[dr * 2 + j][rows, (h % 2) * 128:(h % 2) * 128 + 128], qq_sb[dr][rows, j, tk]) for dr in range(2)],
                         [SINB[0 + j], SINB[2 + j], QQ[0][t], QQ[1][t]], [PS[pb]])
                S.op('dve', lambda e, h=h, tk=tk, pb=pb: e.tensor_tensor(out=o_sb[:, h, tk], in0=o_sb[:, h, tk], in1=psum[pb][:, :], op=ALU.add),
                     reads=[PS[pb], O[h][t]], writes=[O[h][t]])
                S.op('pool', lambda e, h=h, tk=tk, i=i: e.tensor_tensor(out=osq[i], in0=o_sb[:, h, tk], in1=o_sb[:, h, tk], op=ALU.mult),
                     reads=[O[h][t]], writes=[OSQ[i]])
                pn = 2 + i
                mm_group(psum[pn][:, :], [(ones_bf[:, :], osq[i])], [OSQ[i], ONES], [PS[pn]])
                ri = cnt['norm'] % 2
                cnt['norm'] += 1
                S.op('act', lambda e, pn=pn, ri=ri: e.activation(out=rstd_sb[ri][:, :], in_=psum[pn][:, :], func=AF.Ln, bias=EPS, scale=1.0 / 128),
                     reads=[PS[pn]], writes=[RSTD[ri]])
                S.op('act', lambda e, ri=ri: e.activation(out=rstd_sb[ri][:, :], in_=rstd_sb[ri][:, :], func=AF.Exp, scale=-0.5),
                     reads=[RSTD[ri]], writes=[RSTD[ri]])
                S.op('dve', lambda e, h=h, tk=tk, i=i, ri=ri: e.scalar_tensor_tensor(out=ot[i], in0=o_sb[:, h, tk], scalar=consts[:, C_GLAN + h:C_GLAN + h + 1],
                                                                                   in1=rstd_sb[ri][:, :], op0=ALU.mult, op1=ALU.mult),
                     reads=[O[h][t], RSTD[ri], CONSTS], writes=[OT[i]])
                S.op('dve', lambda e, h=h, tk=tk, i=i: e.tensor_tensor(out=rs_sb[:, h, tk], in0=ot[i], in1=rs_sb[:, h, tk], op=ALU.mult),
                     reads=[OT[i], RS[h][t]], writes=[RS[h][t]])
        n = 0
        for t in range(NTT):
            for dc in range(KD):
                pb = 4 + (n % 2)
                n += 1
                tk = tsl(t)
                pairs = [(wout_a[:, kc, dc * 128:(dc + 1) * 128], rs_sb[:, kc, tk]) for kc in range(4)] + \
                        [(wout_b[:, kc, dc * 128:(dc + 1) * 128], co_sb[:, kc, tk]) for kc in range(4)]
                mm_group(psum[pb][:, :], pairs, WOUTK + [RS[kc][t] for kc in range(4)] + [CO[kc][t] for kc in range(4)], [PS[pb]])
                S.op('dve', lambda e, dc=dc, pb=pb, tk=tk: e.tensor_tensor(out=x_sb[:, dc, tk], in0=x_sb[:, dc, tk], in1=psum[pb][:, :], op=ALU.add),
                     reads=[PS[pb], X[dc][t]], writes=[X[dc][t]])

    def emit_xattn():
        AR.reset()
        SQA.reset()
        QB = [[Buf(f"q{f}_{t}") for t in range(NTT)] for f in range(KD)]
        q_sb = AR.alloc([KD, T], BF16, [b for row in QB for b in row])
        WO_, KT, VM, MEM, MN = Buf("wo"), Buf("kt"), Buf("vm"), Buf("mem"), Buf("mn")
        WOK = [Buf(f"wok{k}") for k in range(KD)]
        wo = AR.alloc([KD, D], BF16, WOK)
        kt = AR.alloc([KD, NMEM], BF16, [KT])
        vm = AR.alloc([2, D], BF16, [VM])
        HA.reset()
        mem_sb = HA.alloc([KD, NMEM], F32, [MEM])
        mn = HA.alloc([KD, NMEM], BF16, [MN])
        WKV = [Buf("wkv0"), Buf("wkv1")]
        wkv_sb = [HA.alloc([KD, 512], BF16, [WKV[i]]) for i in range(2)]
        WQ = [Buf("wq0"), Buf("wq1")]
        wq_sb = [AR.alloc([KD, 128], BF16, [WQ[i]]) for i in range(2)]
        PB_, PT_ = [Buf("p0"), Buf("p1")], [Buf("pt0"), Buf("pt1")]
        p_sb = [AR.alloc([4, NMEM], BF16, [PB_[i]]) for i in range(2)]
        pt_sb = [AR.alloc([8, 128], BF16, [PT_[i]]) for i in range(2)]
        SM = [Buf("sm0"), Buf("sm1")]
        sm = [AR.alloc([16], F32, [SM[i]]) for i in range(2)]
        for nm in ("wo", "mem", "wkv0", "wkv1", "wq0", "wq1"):
            dsem(nm)
        S.dma('sp', "mem", mem_sb, memT.rearrange("(k p) m -> p k m", p=128), writes=[MEM])
        for k in range(KD):
            S.dma('pool', dsem(f"wo{k}"), wo[:, k, :], wo_d[:, k * D:(k + 1) * D], writes=[WOK[k]])
        ri = emit_rstd_generic(None, NMEM, 7, [MEM], mem_sb, 1.0 / D)
        for k in range(KD):
            S.op('dve', lambda e, k=k: e.scalar_tensor_tensor(out=mn[:, k, :], in0=mem_sb[:, k, :], scalar=consts[:, C_GMEM + k:C_GMEM + k + 1],
                                                               in1=rstd_sb[ri][:, 0:NMEM], op0=ALU.mult, op1=ALU.mult),
                 reads=[MEM, RSTD[ri], CONSTS], writes=[MN])
        for cchunk in range(4):
            i = cchunk % 2
            S.dma('pool', f"wkv{i}", wkv_sb[i].rearrange("p k n -> p (k n)"), wkv_d[cchunk, :, :], writes=[WKV[i]])
            if cchunk < 2:
                for fcl in range(4):
                    fc = cchunk * 4 + fcl
                    pb = fcl % 2
                    mm_group(psum[pb][:, 0:NMEM], [(wkv_sb[i][:, k, fcl * 128:(fcl + 1) * 128], mn[:, k, :]) for k in range(KD)], [WKV[i], MN], [PS[pb]])
                    S.op('dve', lambda e, fc=fc, pb=pb: e.tensor_copy(out=kt[:, fc, :], in_=psum[pb][:, 0:NMEM]), reads=[PS[pb]], writes=[KT])
            else:
                half = cchunk - 2
                for mt in range(2):
                    pb = 2 + mt
                    mm_group(psum[pb][:, :], [(mn[:, k, mt * 128:(mt + 1) * 128], wkv_sb[i][:, k, :]) for k in range(KD)], [WKV[i], MN], [PS[pb]])
                    S.op('dve', lambda e, mt=mt, half=half, pb=pb: e.tensor_copy(out=vm[:, mt, half * 512:(half + 1) * 512], in_=psum[pb][:, :]),
                         reads=[PS[pb]], writes=[VM])
        HA.reset()
        HA.alloc([KD, T], BF16, H)
        for t in range(NTT):
            emit_norm_to_h(t, C_GX)
        n = 0
        for fc in range(KD):
            i = fc % 2
            S.dma('pool', f"wq{i}", wq_sb[i].rearrange("p k n -> p (k n)"), wq_d[fc, :, :], writes=[WQ[i]])
            for t in range(NTT):
                pb = 4 + (n % 2)
                n += 1
                mm_group(psum[pb][:, :], [(wq_sb[i][:, k, :], h_sb[:, k, tsl(t)]) for k in range(KD)], [WQ[i], H[t]], [PS[pb]])
                S.op('act', lambda e, fc=fc, t=t, pb=pb: e.activation(out=q_sb[:, fc, tsl(t)], in_=psum[pb][:, :], func=AF.Identity, scale=1.0 / 16.0),
                     reads=[PS[pb]], writes=[QB[fc][t]])
        AO = [Buf(f"ao{t}") for t in range(NTT)]
        HA.reset()
        ao = HA.alloc([KD, T], BF16, AO)
        for tb in range(T // 128):
            i = tb % 2
            t = tb // 4
            tok = slice(tb * 128, (tb + 1) * 128)
            sp_ = (0, 1) if i == 0 else (2, 3)
            for hd in range(4):
                pb = sp_[hd // 2]
                mm_group(psum[pb][:, (hd % 2) * 256:(hd % 2 + 1) * 256],
                         [(q_sb[:, 2 * hd + kc, tok], kt[:, 2 * hd + kc, :]) for kc in range(2)], [QB[2 * hd][t], QB[2 * hd + 1][t], KT], [PS[pb]])
            for half in range(2):
                pb = sp_[half]
                S.op('dve', lambda e, pb=pb, half=half, i=i: e.tensor_reduce(out=sm[i][:, 2 * half:2 * half + 2],
                                                                            in_=psum[pb][:, :].rearrange("p (a b) -> p a b", a=2), axis=AX.X, op=ALU.max),
                     reads=[PS[pb]], writes=[SM[i]])
            S.op('dve', lambda e, i=i: e.tensor_scalar(out=sm[i][:, 4:8], in0=sm[i][:, 0:4], scalar1=-1.0, scalar2=None, op0=ALU.mult),
                 reads=[SM[i]], writes=[SM[i]])
            for hd in range(4):
                pb = sp_[hd // 2]
                S.op('act', lambda e, hd=hd, pb=pb, i=i: e.activation(out=p_sb[i][:, hd, :], in_=psum[pb][:, (hd % 2) * 256:(hd % 2 + 1) * 256], func=AF.Exp,
                                                                     bias=sm[i][:, 4 + hd:5 + hd], scale=1.0, accum_out=sm[i][:, 8 + hd:9 + hd]),
                     reads=[PS[pb], SM[i]], writes=[PB_[i], SM[i]])
            S.op('dve', lambda e, i=i: e.reciprocal(out=sm[i][:, 12:16], in_=sm[i][:, 8:12]), reads=[SM[i]], writes=[SM[i]])
            S.op('dve', lambda e, i=i: e.tensor_tensor(out=p_sb[i], in0=p_sb[i], in1=sm[i][:, 12:16].unsqueeze(2).to_broadcast([128, 4, NMEM]), op=ALU.mult),
                 reads=[PB_[i], SM[i]], writes=[PB_[i]])
            pbt = 4 + i
            ptv = psum[pbt][:, :].bitcast(BF16)[:, 0:1024].rearrange("p (a b) -> p a b", a=8)
            for hd in range(4):
                for mt in range(2):
                    S.op('pe', lambda e, hd=hd, mt=mt, i=i, ptv=ptv: e.transpose(out=ptv[:, hd * 2 + mt, :], in_=p_sb[i][:, hd, mt * 128:(mt + 1) * 128],
                                                                                identity=ident_bf[:, :]),
                         reads=[PB_[i], IDENT], writes=[PS[pbt]])
            S.op('act', lambda e, i=i, ptv=ptv: e.copy(out=pt_sb[i], in_=ptv), reads=[PS[pbt]], writes=[PT_[i]])
            pbo = 6 + i
            ov = psum[pbo][:, :].rearrange("p (a b) -> p a b", a=4)
            for half in range(2):
                if half == 1:
                    pbo2 = pbo
                for fcl in range(4):
                    fc = half * 4 + fcl
                    hd = fc // 2
                    mm_group(ov[:, fcl, :], [(vm[:, mt, fc * 128:(fc + 1) * 128], pt_sb[i][:, hd * 2 + mt, :]) for mt in range(2)], [VM, PT_[i]], [PS[pbo]])
                S.op('dve', lambda e, half=half, tok=tok, ov=ov: e.tensor_copy(out=ao[:, half * 4:(half + 1) * 4, tok], in_=ov), reads=[PS[pbo]], writes=[AO[t]])
        n = 0
        for t in range(NTT):
            for dc in range(KD):
                pb = 0 + (n % 2)
                n += 1
                tk = tsl(t)
                mm_group(psum[pb][:, :], [(wo[:, kc, dc * 128:(dc + 1) * 128], ao[:, kc, tk]) for kc in range(KD)], WOK + [AO[t]], [PS[pb]])
                S.op('dve', lambda e, dc=dc, pb=pb, tk=tk: e.tensor_tensor(out=x_sb[:, dc, tk], in0=x_sb[:, dc, tk], in1=psum[pb][:, :], op=ALU.add),
                     reads=[PS[pb], X[dc][t]], writes=[X[dc][t]])
        HA.reset()
        HA.alloc([KD, T], BF16, H)

    def emit_output(gcol):
        outT_v = outT.rearrange("(k p) t -> p k t", p=128)
        dsem("out")
        for t in range(NTT):
            if gcol is not None:
                i = emit_rstd_generic(None, TT, 6 + (cnt['norm'] % 2), [X[k][t] for k in range(KD)], x_sb[:, :, tsl(t)], 1.0 / D)
                for k in range(KD):
                    S.op('dve', lambda e, k=k, t=t, i=i: e.scalar_tensor_tensor(
                        out=x_sb[:, k, tsl(t)], in0=x_sb[:, k, tsl(t)], scalar=consts[:, gcol + k:gcol + k + 1],
                        in1=rstd_sb[i][:, :], op0=ALU.mult, op1=ALU.mult),
                        reads=[X[k][t], RSTD[i], CONSTS], writes=[X[k][t]])
            S.dma('sp', "out", outT_v[:, :, tsl(t)], x_sb[:, :, tsl(t)], reads=[X[k][t] for k in range(KD)])
        S.wait_event('sp', ("out", S.cnt["out"]))

    emit_ffn(0, C_G1)
    if stage >= 1.1:
        M = emit_mixer()
    if stage >= 1.2:
        emit_pass_r(M)
    if stage >= 1.3:
        emit_collective(M)
    if stage >= 2:
        emit_mixer_finish(M)
        HA.reset()
        HA.alloc([KD, T], BF16, H)
    if stage >= 3:
        emit_xattn()
    if stage >= 4:
        emit_ffn(1, C_G2)
    emit_output(C_GF if stage >= 99 else None)

    with nc.Block() as block:
        @block.tensor
        def _(e):
            for f in S.ops['pe']:
                f(e)

        @block.scalar
        def _(e):
            for f in S.ops['act']:
                f(e)

        @block.vector
        def _(e):
            for f in S.ops['dve']:
                f(e)

        @block.gpsimd
        def _(e):
            for f in S.ops['pool']:
                f(e)

        @block.sync
        def _(e):
            for f in S.ops['sp']:
                f(e)
    es.close()
    return nc


def _tile_wgu(wg, wu):
    w = np.stack([wg, wu], axis=0).reshape(2, KD, 128, NF, 128)
    return np.ascontiguousarray(w.transpose(3, 2, 0, 1, 4)).reshape(NF, 128, 2 * KD * 128)


def _ktile(w):
    n = w.shape[1]
    return np.ascontiguousarray(w.reshape(KD, 128, n).transpose(1, 0, 2)).reshape(128, KD * n)


def _gcols(g, n=KD):
    return np.asarray(g, np.float32).reshape(n, 128).T


def make_in_maps(inp):
    f32 = lambda a: np.asarray(a, np.float32)
    x = f32(inp["x"])
    mem = f32(inp["mem"])
    w_in = f32(inp["w_in"][0])
    q0, k0, v0, r0, zf0, zb0, bg0, cg0, xv0 = 0, 256, 512, 1024, 1536, 1552, 1568, 2080, 2592
    wgla = _ktile(np.concatenate([w_in[:, q0:q0 + 256], w_in[:, k0:k0 + 256], w_in[:, v0:v0 + 512],
                                  w_in[:, zf0:zf0 + 16], w_in[:, zb0:zb0 + 16]], axis=1))
    wr = np.stack([_ktile(w_in[:, c0 + c * 128:c0 + (c + 1) * 128]) for c0 in (bg0, cg0, xv0, r0) for c in range(4)], axis=0)
    w2aug = np.zeros((33, 512), np.float32)
    w2aug[0:16, 0:256] = f32(inp["gate_fwd_w"][0])
    w2aug[16:32, 256:512] = f32(inp["gate_bwd_w"][0])
    w2aug[32, 0:256] = f32(inp["gate_fwd_b"][0])
    w2aug[32, 256:512] = f32(inp["gate_bwd_b"][0])
    tp = np.arange(TT)[:, None]
    tc = np.arange(TT)[None, :]
    mats = [(tp <= tc), (tp > tc), (tp >= tc), (tp < tc)]
    tri = np.stack([np.ascontiguousarray((m.astype(np.float32) * np.float32(SNEG)).reshape(4, 128, TT).transpose(1, 0, 2)).reshape(128, 4 * TT)
                    for m in mats], axis=0)
    wgu1 = _tile_wgu(f32(inp["ffn1_w_gate"][0]), f32(inp["ffn1_w_up"][0]))
    wgu2 = _tile_wgu(f32(inp["ffn2_w_gate"][0]), f32(inp["ffn2_w_up"][0]))
    wd1 = np.ascontiguousarray(f32(inp["ffn1_w_down"][0]).reshape(NF, 128, D))
    wd2 = np.ascontiguousarray(f32(inp["ffn2_w_down"][0]).reshape(NF, 128, D))
    wout = _ktile(f32(inp["w_out"][0]))
    wq_full = f32(inp["xattn_w_q"][0])
    wq = np.stack([_ktile(wq_full[:, fc * 128:(fc + 1) * 128]) for fc in range(KD)], axis=0)
    wkv_full = f32(inp["xattn_w_kv"][0])
    wkv = np.stack([_ktile(wkv_full[:, c * 512:(c + 1) * 512]) for c in range(4)], axis=0)
    wo = _ktile(f32(inp["xattn_w_o"][0]))
    ident = np.eye(128, dtype=np.float32)

    base = np.zeros((128, NCONST), np.float32)
    base[:, C_G1:C_G1 + 8] = _gcols(inp["ffn1_norm"][0])
    base[:, C_G2:C_G2 + 8] = _gcols(inp["ffn2_norm"][0])
    base[:, C_GF:C_GF + 8] = _gcols(inp["final_norm"])
    base[:, C_GM:C_GM + 8] = _gcols(inp["mix_norm"][0])
    base[:, C_GX:C_GX + 8] = _gcols(inp["xattn_norm"][0])
    base[:, C_GMEM:C_GMEM + 8] = _gcols(inp["mem_norm"][0])
    base[:, C_GLAN:C_GLAN + 4] = _gcols(inp["gla_norm"][0], 4)
    cw = f32(inp["conv_w"][0])
    base[:, C_CW0:C_CW0 + 4] = _gcols(cw[0], 4)
    base[:, C_CW1:C_CW1 + 4] = _gcols(cw[1], 4)
    base[:, C_CW2:C_CW2 + 4] = _gcols(cw[2], 4)
    base[:, C_CB:C_CB + 4] = _gcols(inp["conv_b"][0], 4)
    maps = []
    for c in range(NCORES):
        b, q = divmod(c, 4)
        cst = base.copy()
        for dr in range(2):
            for r in range(NCORES):
                same = (r // 4 == b)
                before = same and ((r < c) if dr == 0 else (r > c))
                cst[:, C_MASK + dr * 8 + r] = 1.0 if before else 0.0
                for m in range(NCORES):
                    between = before and (m // 4 == b) and ((r < m < c) if dr == 0 else (c < m < r))
                    cst[:, C_B + dr * 64 + r * 8 + m] = 1.0 if between else 0.0
        for r in range(NCORES):
            cst[:, C_SELP + r] = 1.0 if (r // 4 == b and r == c - 1) else 0.0
            cst[:, C_SELN + r] = 1.0 if (r // 4 == b and r == c + 1) else 0.0
        xs = x[b, q * T:(q + 1) * T, :]
        maps.append({
            "xT": np.ascontiguousarray(xs.T),
            "memT": np.ascontiguousarray(mem[b].T),
            "consts": cst, "ident": ident,
            "wgu1": wgu1, "wgu2": wgu2, "wd1": wd1, "wd2": wd2,
            "wgla": wgla, "wr": wr, "w2aug": w2aug, "tri": tri, "wout": wout,
            "wq": wq, "wkv": wkv, "wo": wo,
        })
    return maps


def run(inp, stage=99, trace=False):
    nc = build_nc(stage)
    maps = make_in_maps(inp)
    res = run_bass_kernel_spmd(nc, maps, core_ids=list(range(NCORES)), trace=trace)
    out = np.empty((2, 4 * T, D), np.float32)
    for c in range(NCORES):
        b, q = divmod(c, 4)
        out[b, q * T:(q + 1) * T, :] = np.asarray(res.results[c]["outT"]).T
    return out, res


def kernel(**inputs):
    out, _ = run(inputs, 99)
    return out
```
